# Optimizing a Trainium2 kernel written in Bass

```python
import jax, jax.numpy as jnp
from jax import lax
import numpy as np

D_MODEL = 2048
BATCH = 4
SEQ = 4096
DEPTH = 4

GRID_W = 64
CTX_LEN = 256
N_MIXERS = 2
N_LAYERS_NA = (DEPTH + 1) // 2
N_LAYERS_GDN = DEPTH // 2

NA_HEADS = 16
NA_HEAD_DIM = D_MODEL // NA_HEADS
NA_WIN_H = 8
NA_WIN_W = 16
NA_SCALE = NA_HEAD_DIM ** -0.5

GDN_K_HEADS = 16
GDN_V_HEADS = 32
GDN_HEAD_K = 128
GDN_HEAD_V = 128
GDN_KEY_DIM = GDN_K_HEADS * GDN_HEAD_K
GDN_VALUE_DIM = GDN_V_HEADS * GDN_HEAD_V
GDN_CONV_DIM = 2 * GDN_KEY_DIM + GDN_VALUE_DIM
GDN_CONV = 4
GDN_CHUNK = 64
GDN_SCALE = GDN_HEAD_K ** -0.5

D_FF = -(-8 * D_MODEL // (3 * 256)) * 256

ADA_SCALE = 0.5
EPS = 1e-6
NEG_INF = -1e30

kernel_name = "hybrid_natten_gdeltanet_dit"


def rms_norm(x, w):
    xf = x.astype(jnp.float32)
    y = xf * lax.rsqrt(jnp.mean(xf * xf, axis=-1, keepdims=True) + EPS)
    return (y * w.astype(jnp.float32)).astype(x.dtype)


def l2_norm(x):
    xf = x.astype(jnp.float32)
    return xf * lax.rsqrt(jnp.sum(xf * xf, axis=-1, keepdims=True) + EPS)


def modulate(h, shift, scale):
    return h * (1 + scale) + shift


def swiglu(h, w1, w3, w2):
    return (jax.nn.silu(h @ w1) * (h @ w3)) @ w2


def centred_dwconv(x, w):
    k = w.shape[0]
    return lax.conv_general_dilated(
        x, w[:, None, :], window_strides=(1,), padding=(((k - 1) // 2, k // 2),),
        dimension_numbers=("NWC", "WIO", "NWC"), feature_group_count=x.shape[-1])


def na_mixer(h, hc, w_qkv, w_o, q_gain, k_gain, rpb, update_ctx):
    bsz, seq, _ = h.shape
    rows = seq // GRID_W
    kh = min(NA_WIN_H, rows)

    def project(t):
        qkv = (t @ w_qkv).reshape(t.shape[0], t.shape[1], 3, NA_HEADS, NA_HEAD_DIM)
        return rms_norm(qkv[:, :, 0], q_gain), rms_norm(qkv[:, :, 1], k_gain), qkv[:, :, 2]

    q, k, v = project(h)
    qc, kc, vc = project(hc)

    def to_grid(t):
        return t.reshape(bsz, rows, GRID_W, NA_HEADS, NA_HEAD_DIM).transpose(0, 3, 1, 2, 4)

    q, k, v = to_grid(q), to_grid(k), to_grid(v)
    kc_h = kc.transpose(0, 2, 1, 3)
    vc_h = vc.transpose(0, 2, 1, 3)

    cols = jnp.arange(GRID_W)
    col_start = jnp.clip(cols - NA_WIN_W // 2, 0, GRID_W - NA_WIN_W)
    in_win = (cols[None, :] >= col_start[:, None]) & (cols[None, :] < col_start[:, None] + NA_WIN_W)
    dc = jnp.clip(cols[None, :] - cols[:, None], 1 - NA_WIN_W, NA_WIN_W - 1) + NA_WIN_W - 1
    bias_cols = jnp.where(in_win[None, None], rpb[:, :, dc].astype(jnp.float32), NEG_INF)

    def row_block(r):
        rs = jnp.clip(r - kh // 2, 0, rows - kh)
        q_r = lax.dynamic_index_in_dim(q, r, axis=2, keepdims=False)
        k_b = lax.dynamic_slice_in_dim(k, rs, kh, axis=2)
        v_b = lax.dynamic_slice_in_dim(v, rs, kh, axis=2)
        bias = jnp.take(bias_cols, rs - r + jnp.arange(kh) + NA_WIN_H - 1, axis=1)
        s_loc = jnp.einsum("bhqd,bhjkd->bhqjk", q_r, k_b,
                           preferred_element_type=jnp.float32) * NA_SCALE + bias.transpose(0, 2, 1, 3)
        s_ctx = jnp.einsum("bhqd,bhcd->bhqc", q_r, kc_h,
                           preferred_element_type=jnp.float32) * NA_SCALE
        s = jnp.concatenate([s_loc.reshape(bsz, NA_HEADS, GRID_W, kh * GRID_W), s_ctx], axis=-1)
        p = jax.nn.softmax(s, axis=-1).astype(v.dtype)
        p_loc = p[..., :kh * GRID_W].reshape(bsz, NA_HEADS, GRID_W, kh, GRID_W)
        return (jnp.einsum("bhqjk,bhjkd->bhqd", p_loc, v_b)
                + jnp.einsum("bhqc,bhcd->bhqd", p[..., kh * GRID_W:], vc_h))

    o = lax.map(row_block, jnp.arange(rows))
    y = o.transpose(1, 0, 3, 2, 4).reshape(bsz, seq, D_MODEL) @ w_o

    yc = None
    if update_ctx:
        sc = jnp.einsum("bqhd,bkhd->bhqk", qc, kc, preferred_element_type=jnp.float32) * NA_SCALE
        pc = jax.nn.softmax(sc, axis=-1).astype(vc.dtype)
        yc = jnp.einsum("bhqk,bkhd->bqhd", pc, vc).reshape(bsz, hc.shape[1], D_MODEL) @ w_o
    return y, yc


def chunk_gated_delta(q, k, v, g, beta, state0):
    bsz, nh, tlen, dk = q.shape
    n = tlen // GDN_CHUNK
    cl = GDN_CHUNK

    def chunks(t):
        return t.reshape(bsz, nh, n, cl, *t.shape[3:])

    q, k, v, g, beta = chunks(q), chunks(k), chunks(v), chunks(g), chunks(beta)
    g = jnp.cumsum(g, axis=-1)
    lower = jnp.tril(jnp.ones((cl, cl), dtype=bool))
    strict = jnp.tril(jnp.ones((cl, cl), dtype=bool), -1)
    diff = g[..., :, None] - g[..., None, :]
    decay = jnp.where(lower, jnp.exp(jnp.where(lower, diff, 0.0)), 0.0)
    k_beta = k * beta[..., None]
    v_beta = v * beta[..., None]
    l_mat = jnp.where(strict, jnp.einsum("bhnid,bhnjd->bhnij", k_beta, k) * decay, 0.0)
    eye = jnp.eye(cl, dtype=jnp.float32)
    t_inv = lax.linalg.triangular_solve(eye + l_mat, jnp.broadcast_to(eye, l_mat.shape),
                                        left_side=True, lower=True, unit_diagonal=True)
    u = jnp.einsum("bhnij,bhnjd->bhnid", t_inv, v_beta)
    w = jnp.einsum("bhnij,bhnjd->bhnid", t_inv, k_beta * jnp.exp(g)[..., None])
    a_intra = jnp.where(lower, jnp.einsum("bhnid,bhnjd->bhnij", q, k) * decay, 0.0)
    g_last = g[..., -1]
    k_dec = k * jnp.exp(g_last[..., None] - g)[..., None]
    q_dec = q * jnp.exp(g)[..., None]

    def step(s, xs):
        u_i, w_i, q_i, kd_i, a_i, gl_i = xs
        v_new = u_i - jnp.einsum("bhcd,bhde->bhce", w_i, s)
        o = jnp.einsum("bhcd,bhde->bhce", q_i, s) + jnp.einsum("bhcs,bhse->bhce", a_i, v_new)
        s = s * jnp.exp(gl_i)[..., None, None] + jnp.einsum("bhcd,bhce->bhde", kd_i, v_new)
        return s, o

    xs = tuple(jnp.moveaxis(t, 2, 0) for t in (u, w, q_dec, k_dec, a_intra, g_last))
    s_final, o = lax.scan(step, state0, xs)
    return s_final, jnp.moveaxis(o, 0, 2).reshape(bsz, nh, tlen, v.shape[-1])


def gdn_mixer(h, hc, w_in, conv_w, w_ba, a_log, dt_bias, norm_w, w_o, update_ctx):
    rep = GDN_V_HEADS // GDN_K_HEADS

    def project(t):
        bsz, tlen, _ = t.shape
        qkvz = t @ w_in
        qkv = jax.nn.silu(centred_dwconv(qkvz[..., :GDN_CONV_DIM], conv_w))
        z = qkvz[..., GDN_CONV_DIM:]
        q = qkv[..., :GDN_KEY_DIM].reshape(bsz, tlen, GDN_K_HEADS, GDN_HEAD_K)
        k = qkv[..., GDN_KEY_DIM:2 * GDN_KEY_DIM].reshape(bsz, tlen, GDN_K_HEADS, GDN_HEAD_K)
        v = qkv[..., 2 * GDN_KEY_DIM:].reshape(bsz, tlen, GDN_V_HEADS, GDN_HEAD_V)
        q = jnp.repeat(l2_norm(q) * GDN_SCALE, rep, axis=2)
        k = jnp.repeat(l2_norm(k), rep, axis=2)
        ba = jnp.einsum("btd,zde->zbte", t, w_ba).astype(jnp.float32)
        beta = jax.nn.sigmoid(ba[..., :GDN_V_HEADS])
        g = -jnp.exp(a_log.astype(jnp.float32))[:, None, None, :] * jax.nn.softplus(
            ba[..., GDN_V_HEADS:] + dt_bias.astype(jnp.float32)[:, None, None, :])
        return (jnp.swapaxes(q, 1, 2), jnp.swapaxes(k, 1, 2),
                jnp.swapaxes(v.astype(jnp.float32), 1, 2), z,
                jnp.swapaxes(g, 2, 3), jnp.swapaxes(beta, 2, 3))

    def gated_out(o, z):
        bsz, _, tlen, _ = o.shape
        zz = z.reshape(bsz, tlen, GDN_V_HEADS, GDN_HEAD_V).astype(jnp.float32)
        y = rms_norm(jnp.swapaxes(o, 1, 2), norm_w) * jax.nn.silu(zz)
        return y.reshape(bsz, tlen, GDN_VALUE_DIM).astype(h.dtype) @ w_o

    def flip(t):
        return jnp.flip(t, axis=2)

    q, k, v, z, g, beta = project(h)
    qc, kc, vc, zc, gc, betac = project(hc)
    s0 = jnp.zeros((h.shape[0], GDN_V_HEADS, GDN_HEAD_K, GDN_HEAD_V), jnp.float32)
    sc_f, oc_f = chunk_gated_delta(qc, kc, vc, gc[0], betac[0], s0)
    sc_b, oc_b = chunk_gated_delta(flip(qc), flip(kc), flip(vc), flip(gc[1]), flip(betac[1]), s0)
    _, o_f = chunk_gated_delta(q, k, v, g[0], beta[0], sc_f)
    _, o_b = chunk_gated_delta(flip(q), flip(k), flip(v), flip(g[1]), flip(beta[1]), sc_b)
    y = gated_out(o_f + flip(o_b), z)
    yc = gated_out(oc_f + flip(oc_b), zc) if update_ctx else None
    return y, yc


def setup_inputs(seed: int = 0) -> dict:
    key = jax.random.key(seed)
    ks = jax.random.split(key, 24)

    def nrm(k, shape, scale):
        return jax.random.normal(k, shape, jnp.float32) * scale

    d = D_MODEL
    dt = jnp.exp(jax.random.uniform(ks[17], (N_LAYERS_GDN, 2, GDN_V_HEADS), jnp.float32,
                                    minval=np.log(1e-3), maxval=np.log(1e-1)))
    return {
        "x": nrm(ks[0], (BATCH, SEQ, d), 1.0),
        "c": nrm(ks[1], (BATCH, d), 1.0),
        "ctx": nrm(ks[2], (BATCH, CTX_LEN, d), 1.0),
        "c_ctx": nrm(ks[3], (d,), 1.0),
        "w_ada": nrm(ks[4], (DEPTH, d, 6 * d), ADA_SCALE * d ** -0.5),
        "b_ada": nrm(ks[5], (DEPTH, 6 * d), 0.02),
        "norm1_w": 1.0 + nrm(ks[6], (DEPTH, d), 0.02),
        "norm2_w": 1.0 + nrm(ks[7], (DEPTH, d), 0.02),
        "na_w_qkv": nrm(ks[8], (N_LAYERS_NA, d, 3 * d), d ** -0.5),
        "na_w_o": nrm(ks[9], (N_LAYERS_NA, d, d), d ** -0.5),
        "na_q_gain": 1.0 + nrm(ks[10], (N_LAYERS_NA, NA_HEAD_DIM), 0.02),
        "na_k_gain": 1.0 + nrm(ks[11], (N_LAYERS_NA, NA_HEAD_DIM), 0.02),
        "na_rpb": nrm(ks[12], (N_LAYERS_NA, NA_HEADS, 2 * NA_WIN_H - 1, 2 * NA_WIN_W - 1), 0.5),
        "gdn_w_in": nrm(ks[13], (N_LAYERS_GDN, d, GDN_CONV_DIM + GDN_VALUE_DIM), d ** -0.5),
        "gdn_conv_w": nrm(ks[14], (N_LAYERS_GDN, GDN_CONV, GDN_CONV_DIM), GDN_CONV ** -0.5),
        "gdn_w_ba": nrm(ks[15], (N_LAYERS_GDN, 2, d, 2 * GDN_V_HEADS), d ** -0.5),
        "gdn_a_log": jnp.log(jax.random.uniform(ks[16], (N_LAYERS_GDN, 2, GDN_V_HEADS), jnp.float32,
                                                minval=1.0, maxval=16.0)),
        "gdn_dt_bias": jnp.log(jnp.expm1(dt)),
        "gdn_norm_w": 1.0 + nrm(ks[18], (N_LAYERS_GDN, GDN_HEAD_V), 0.02),
        "gdn_w_o": nrm(ks[19], (N_LAYERS_GDN, GDN_VALUE_DIM, d), GDN_VALUE_DIM ** -0.5),
        "ffn_w1": nrm(ks[20], (DEPTH, d, D_FF), d ** -0.5),
        "ffn_w3": nrm(ks[21], (DEPTH, d, D_FF), d ** -0.5),
        "ffn_w2": nrm(ks[22], (DEPTH, D_FF, d), D_FF ** -0.5),
    }


def reference(x, c, ctx, c_ctx, w_ada, b_ada, norm1_w, norm2_w, na_w_qkv, na_w_o, na_q_gain,
              na_k_gain, na_rpb, gdn_w_in, gdn_conv_w, gdn_w_ba, gdn_a_log, gdn_dt_bias,
              gdn_norm_w, gdn_w_o, ffn_w1, ffn_w3, ffn_w2):
    xc = ctx
    silu_c = jax.nn.silu(c)
    silu_cc = jax.nn.silu(c_ctx)
    for i in range(DEPTH):
        update_ctx = i < DEPTH - 1
        mod = (silu_c @ w_ada[i] + b_ada[i])[:, None, :]
        mod_c = (silu_cc @ w_ada[i] + b_ada[i])[None, None, :]
        sh1, sc1, gt1, sh2, sc2, gt2 = jnp.split(mod, 6, axis=-1)
        csh1, csc1, cgt1, csh2, csc2, cgt2 = jnp.split(mod_c, 6, axis=-1)

        h = modulate(rms_norm(x, norm1_w[i]), sh1, sc1)
        hc = modulate(rms_norm(xc, norm1_w[i]), csh1, csc1)
        j = i // N_MIXERS
        if i % N_MIXERS == 0:
            y, yc = na_mixer(h, hc, na_w_qkv[j], na_w_o[j], na_q_gain[j], na_k_gain[j],
                             na_rpb[j], update_ctx)
        else:
            y, yc = gdn_mixer(h, hc, gdn_w_in[j], gdn_conv_w[j], gdn_w_ba[j], gdn_a_log[j],
                              gdn_dt_bias[j], gdn_norm_w[j], gdn_w_o[j], update_ctx)

        x = x + gt1 * y
        x = x + gt2 * swiglu(modulate(rms_norm(x, norm2_w[i]), sh2, sc2),
                             ffn_w1[i], ffn_w3[i], ffn_w2[i])
        if update_ctx:
            xc = xc + cgt1 * yc
            xc = xc + cgt2 * swiglu(modulate(rms_norm(xc, norm2_w[i]), csh2, csc2),
                                    ffn_w1[i], ffn_w3[i], ffn_w2[i])
    return x
```

```python
import contextlib
import os
import numpy as np
import concourse.bass as bass
import concourse.mybir as mybir
from concourse.bass_utils import run_bass_kernel_spmd

F32 = mybir.dt.float32
BF16 = mybir.dt.bfloat16
AF = mybir.ActivationFunctionType
ALU = mybir.AluOpType
AX = mybir.AxisListType

D = 2048
SEQ = 4096
CTX = 256
NT = SEQ + CTX
NTILE = NT // 128
DFF = 5632
EPS = 1e-6
NEG = -1e30
ENGS = ['pe', 'act', 'dve', 'pool', 'sp']
NDSEM = 8


class _Rec:
    def __init__(self):
        self.call = None

    def __getattr__(self, name):
        def f(*args, **kwargs):
            self.call = (name, args, kwargs)
            return None
        return f


class Prog:
    def __init__(self, nc):
        self.nc = nc
        self.ops = {e: [] for e in ENGS}
        self.last_w = {}
        self.rd_c = {}
        self.rd_d = {}
        self.ndma = {e: 0 for e in ENGS}
        self.pending = {e: set() for e in ENGS}
        self.dma_since = []

    def op(self, eng, fn, reads=(), writes=(), dma=False):
        rec = _Rec()
        fn(rec)
        name, args, kwargs = rec.call
        fn = (lambda name=name, args=args, kwargs=kwargs: lambda e: getattr(e, name)(*args, **kwargs))()
        ops = self.ops[eng]
        idx = len(ops)
        deps = set(self.pending[eng])
        self.pending[eng] = set()
        for k in reads:
            w = self.last_w.get(k)
            if w is not None:
                deps.add(w)
        for k in writes:
            w = self.last_w.get(k)
            if w is not None:
                deps.add(w)
            for e2, j in self.rd_c.get(k, {}).items():
                deps.add((e2, j))
            for r in self.rd_d.get(k, ()):
                deps.add(r)
        for k in reads:
            if dma:
                self.rd_d.setdefault(k, []).append((eng, idx))
            else:
                self.rd_c.setdefault(k, {})[eng] = idx
        for k in writes:
            self.last_w[k] = (eng, idx)
            self.rd_c[k] = {}
            self.rd_d[k] = []
        deps.discard((eng, idx))
        dk = None
        if dma:
            dk = self.ndma[eng]
            self.ndma[eng] += 1
            self.dma_since.append((eng, idx))
        ops.append(dict(fn=fn, deps=deps, dma=dma, dk=dk, marked=False))
        return (eng, idx)

    def barrier(self):
        s = set(self.dma_since)
        self.dma_since = []
        for e in ENGS:
            if self.ops[e]:
                s.add((e, len(self.ops[e]) - 1))
        for e in ENGS:
            self.pending[e] |= s
        self.last_w = {}
        self.rd_c = {}
        self.rd_d = {}

    def emit(self, final_wait=()):
        nc = self.nc
        ops = self.ops
        for e in ENGS:
            for i, o in enumerate(ops[e]):
                cm = {}
                dd = []
                for (e2, j) in o['deps']:
                    if e2 == e and j >= i:
                        continue
                    if ops[e2][j]['dma']:
                        dd.append((e2, j))
                    else:
                        if e2 == 'pe' and e == 'pe' and not o['dma']:
                            continue
                        if j > cm.get(e2, -1):
                            cm[e2] = j
                o['cdeps'] = cm
                o['ddeps'] = dd
                for e2, j in cm.items():
                    ops[e2][j]['marked'] = True
        cnt = {}
        for e in ENGS:
            c = 0
            arr = []
            for o in ops[e]:
                if o['marked'] and not o['dma']:
                    c += 1
                arr.append(c)
            cnt[e] = arr
        with contextlib.ExitStack() as st:
            csem = {e: st.enter_context(nc.semaphore("cs_" + e)) for e in ENGS}
            dsem = {e: [st.enter_context(nc.semaphore("ds_%s%d" % (e, i))) for i in range(NDSEM)]
                    for e in ENGS if self.ndma[e] > 0}
            block = st.enter_context(nc.Block())

            def dma_target(e2, j):
                dk = ops[e2][j]['dk']
                return dsem[e2][dk % NDSEM], 16 * (dk // NDSEM + 1)

            def body(e, eng):
                waited = {}

                def wait(sem, key, val):
                    if waited.get(key, 0) >= val:
                        return
                    waited[key] = val
                    eng.wait_ge(sem, val)

                for i, o in enumerate(ops[e]):
                    for e2, j in o['cdeps'].items():
                        wait(csem[e2], ('c', e2), cnt[e2][j])
                    for (e2, j) in o['ddeps']:
                        s, v = dma_target(e2, j)
                        wait(s, ('d', e2, ops[e2][j]['dk'] % NDSEM), v)
                    if o['dma']:
                        dk = o['dk']
                        if dk >= NDSEM:
                            wait(dsem[e][dk % NDSEM], ('d', e, dk % NDSEM), 16 * (dk // NDSEM))
                    ins = o['fn'](eng)
                    if o['dma']:
                        ins.then_inc(dsem[e][o['dk'] % NDSEM], 16)
                    elif o['marked']:
                        ins.then_inc(csem[e], 1)
                if e == 'sp':
                    for (e2, j) in final_wait:
                        s, v = dma_target(e2, j)
                        wait(s, ('d', e2, ops[e2][j]['dk'] % NDSEM), v)

            @block.tensor
            def _(eng):
                body('pe', eng)

            @block.scalar
            def _(eng):
                body('act', eng)

            @block.vector
            def _(eng):
                body('dve', eng)

            @block.gpsimd
            def _(eng):
                body('pool', eng)

            @block.sync
            def _(eng):
                body('sp', eng)


def groups_of_tiles(include_ctx=True):
    g = [(4 * i, 4) for i in range(8)]
    if include_ctx:
        g.append((32, 2))
    return g


class Builder:
    def __init__(self, nlayers, debug_out=None):
        self.nl = nlayers
        nc = self.nc = bass.Bass("TRN2", target_bir_lowering=False)
        self.P = Prog(nc)
        self.uid = 0
        self.in_names = []

        self.in_shapes = {}
        mini = os.environ.get("MK_MINI", "")
        keep = set(mini.split(",")) if mini else None

        def ein(n, s):
            if keep is not None and n not in keep:
                s = [1, 1]
            self.in_names.append(n)
            self.in_shapes[n] = list(s)
            return nc.dram_tensor(n, s, F32, kind="ExternalInput").ap()
        nl = nlayers
        nna = (nl + 1) // 2
        ngd = nl // 2
        self.x_in = ein("x_in", [NT, D])
        self.cvec = ein("cvec", [128, 16, 2])
        self.w_ada = [ein("w_ada%d" % i, [D, 6 * D]) for i in range(nl)]
        self.b_ada = ein("b_ada", [4, 6 * D])
        self.norm1 = ein("norm1_w", [4, D])
        self.norm2 = ein("norm2_w", [4, D])
        self.na_wqkv = [ein("na_w_qkv%d" % j, [D, 3 * D]) for j in range(nna)]
        self.na_wo = [ein("na_w_o%d" % j, [D, D]) for j in range(nna)]
        self.na_qg = ein("na_qg", [2, 128, 1])
        self.na_kg = ein("na_kg", [2, 128, 1])
        self.na_bias = [ein("na_bias%d" % j, [16, 128, 21, 128]) for j in range(nna)]
        self.g_win = [ein("gdn_w_in%d" % j, [D, 12288]) for j in range(ngd)]
        self.g_conv = ein("gdn_conv", [2, 128, 64, 4])
        self.g_wba = ein("gdn_wba", [2, D, 128])
        self.g_alog = ein("gdn_a_log", [2, 64])
        self.g_dtb = ein("gdn_dt_bias", [2, 64])
        self.g_nw = ein("gdn_norm_w", [2, 128])
        self.g_wo = [ein("gdn_w_o%d" % j, [4096, D]) for j in range(ngd)]
        self.w1 = [ein("ffn_w1%d" % i, [D, DFF]) for i in range(nl)]
        self.w3 = [ein("ffn_w3%d" % i, [D, DFF]) for i in range(nl)]
        self.w2 = [ein("ffn_w2%d" % i, [DFF, D]) for i in range(nl)]
        self.consts = ein("consts", [128, 8, 128])
        self.out = nc.dram_tensor("out", [SEQ, D], F32, kind="ExternalOutput").ap()
        dr = lambda n, s, d=F32: nc.dram_tensor(n, s, d).ap()
        self.xs = dr("xs", [NT, D])
        self.modrow = dr("modrow", [2, 6 * D])
        self.wb_a = dr("wb_a", [D, 12288], BF16)
        self.wb_o = dr("wb_o", [4096, D], BF16)
        self.wb_1 = dr("wb_1", [D, DFF], BF16)
        self.wb_3 = dr("wb_3", [D, DFF], BF16)
        self.wb_2 = dr("wb_2", [DFF, D], BF16)
        self.wb_ba = dr("wb_ba", [D, 128], BF16)
        self.QT = dr("QT", [16, 128, NT], BF16)
        self.KT = dr("KT", [16, 128, NT], BF16)
        self.V = dr("V", [NT, D], BF16)
        self.OT = dr("OT", [16, 128, NT], BF16)
        if nlayers >= 2:
            self.PRE = dr("PRE", [64, 128, 4360])
            self.ZTM = dr("ZTM", [NT, 4096])
            self.G = dr("Gs", [NT, 64])
            self.BETA = dr("BETAs", [NT, 64])
            self.QTg = dr("QTg", [16, 128, NT])
            self.KTg = dr("KTg", [16, 128, NT])
            self.KTM = dr("KTM", [NT, 2048])
            self.VTM = dr("VTM", [NT, 4096])
            self.OF = dr("OF", [NT, 4096])
            self.OB = dr("OB", [NT, 4096])
            self.YTg = dr("YTg", [32, 128, NT], BF16)
        self.debug_out = debug_out

    def nm(self, s):
        self.uid += 1
        return "%s_%d" % (s, self.uid)

    def dma(self, out, in_, reads=(), writes=(), q='sp'):
        return self.P.op(q, lambda e: e.dma_start(out=out, in_=in_), reads=reads, writes=writes, dma=True)

    def convert_weight(self, dst, src, rows, cols, key):
        rb = 64
        for r0 in range(0, rows, rb):
            r1 = min(rows, r0 + rb)
            self.dma(dst[r0:r1, 0:cols], src[r0:r1, 0:cols], writes=[(key, r0)], q='pool')
        return [(key, r0) for r0 in range(0, rows, rb)]

    def build(self):
        nc, P = self.nc, self.P
        with contextlib.ExitStack() as gst:
            self.gst = gst
            T = lambda n, s, d=F32: gst.enter_context(nc.sbuf_tensor(n, s, d))
            self.cst = T("cst", [128, 8, 128])
            self.cstb = T("cstb", [128, 2, 128], BF16)
            self.ps = [gst.enter_context(nc.psum_tensor("psb%d" % i, [128, 512], F32)) for i in range(6)]
            self.pst = [gst.enter_context(nc.psum_tensor("pst%d" % i, [128, 1024], BF16)) for i in range(2)]
            self.scv = T("scv", [128, 16, 2], BF16)
            self.epsc = T("epsc", [128, 1])
            P.op('pool', lambda e: e.memset(self.epsc[:], EPS), writes=['epsc'])
            self.dma(self.cst[:], self.consts[:, :, :], writes=['cst'])
            P.op('dve', lambda e: e.tensor_copy(out=self.cstb[:], in_=self.cst[:, 0:2, :]), reads=['cst'], writes=['cstb'])
            cv = T("cvtmp", [128, 16, 2])
            self.dma(cv[:], self.cvec[:, :, :], writes=['cv'])
            P.op('act', lambda e: e.activation(out=self.scv[:], in_=cv[:], func=AF.Silu), reads=['cv'], writes=['scv'])
            for t in range(NTILE):
                if self.in_shapes["x_in"] == [1, 1]:
                    break
                self.dma(self.xs[t * 128:(t + 1) * 128, :], self.x_in[t * 128:(t + 1) * 128, :], writes=[('xs', t)])
            P.barrier()
            for i in range(self.nl):
                self.layer(i)
            P.barrier()
            fin = [self.dma(self.out[t * 128:(t + 1) * 128, :], self.xs[t * 128:(t + 1) * 128, :]) for t in range(32)]
            P.emit(final_wait=fin)
        return nc

    def layer(self, i):
        j = i // 2
        stop = int(os.environ.get("MK_STOP", "99"))
        self.stop = stop
        self.only = os.environ.get("MK_ONLY", "")
        if self.only:
            if self.only.startswith('p') and i % 2 == 0:
                self.na_layer(i, j)
            if self.only.startswith('g') and i % 2 == 1:
                self.gdn_layer(i, j)
            return
        if stop < 1:
            return
        self.mod_phase(i)
        if stop < 2:
            return
        if i % 2 == 0:
            self.na_layer(i, j)
        else:
            self.gdn_layer(i, j)
        if stop < 5:
            return
        self.ffn_phase(i)

    def mod_phase(self, i):
        nc, P = self.nc, self.P
        with contextlib.ExitStack() as st:
            T = lambda n, s, d=F32: st.enter_context(nc.sbuf_tensor(self.nm(n), s, d))
            wt = [T("mw", [128, 16, 512], BF16) for _ in range(2)]
            bt = [T("mb", [1, 512], BF16) for _ in range(2)]
            so = [T("mo", [2, 512]) for _ in range(2)]
            wv = self.w_ada[i].rearrange("(k p) n -> p k n", p=128)
            for n in range(24):
                b = n % 2
                self.dma(wt[b][:], wv[:, :, n * 512:(n + 1) * 512], writes=[('mw', b)], q='pool')
                self.dma(bt[b][:], self.b_ada[i:i + 1, n * 512:(n + 1) * 512], writes=[('mb', b)], q='pool')
                pst = self.ps[b]
                for k in range(16):
                    P.op('pe', (lambda k=k, b=b, pst=pst: lambda e: e.matmul(pst[0:2, :], self.scv[:, k, :], wt[b][:, k, :], start=(k == 0), stop=False))(),
                         reads=['scv', ('mw', b)], writes=[('ps', b)])
                P.op('pe', (lambda b=b, pst=pst: lambda e: e.matmul(pst[0:2, :], self.cstb[0:1, 1, 0:2], bt[b][:], start=False, stop=True))(),
                     reads=['cstb', ('mb', b)], writes=[('ps', b)])
                P.op('act', (lambda b=b, pst=pst: lambda e: e.activation(out=so[b][:], in_=pst[0:2, :], func=AF.Copy))(),
                     reads=[('ps', b)], writes=[('mo', b)])
                self.dma(self.modrow[:, n * 512:(n + 1) * 512], so[b][:], reads=[('mo', b)], writes=[('modrow', n)])
        P.barrier()

    def alloc_vec(self, T):
        return dict(nw=T("nw", [128, D]), A=T("vA", [128, D]), B=T("vB", [128, D]), G=T("vG", [128, D]), cur=None)

    def load_vec(self, vt, i, which, s):
        nc, P = self.nc, self.P
        if vt['cur'] == (i, which, s):
            return
        vt['cur'] = (i, which, s)
        nw, A, B, G = vt['nw'], vt['A'], vt['B'], vt['G']
        normw = self.norm1 if which == 0 else self.norm2
        self.dma(nw[:], normw[i:i + 1, :].partition_broadcast(128), writes=['v_nw'])
        o = which * 3 * D
        self.dma(B[:], self.modrow[s:s + 1, o:o + D].partition_broadcast(128), writes=['v_B'])
        self.dma(A[:], self.modrow[s:s + 1, o + D:o + 2 * D].partition_broadcast(128), writes=['v_A'])
        self.dma(G[:], self.modrow[s:s + 1, o + 2 * D:o + 3 * D].partition_broadcast(128), writes=['v_G'])
        P.op('dve', lambda e: e.scalar_tensor_tensor(out=A[:], in0=A[:], scalar=1.0, in1=nw[:], op0=ALU.add, op1=ALU.mult),
             reads=['v_nw', 'v_A'], writes=['v_A'])

    def ln_tile(self, T, bufs, t, vec, hT, hT_key, col):
        nc, P = self.nc, self.P
        xt, sq, hb, ssq, rstd = bufs['xt'], bufs['sq'], bufs['hb'], bufs['ssq'], bufs['rstd']
        b = bufs['n'] % 2
        bufs['n'] += 1
        A, B, kA, kB = vec['A'], vec['B'], 'v_A', 'v_B'
        kx = ('xt', b)
        self.dma(xt[b][:], self.xs[t * 128:(t + 1) * 128, :], reads=[('xs', t)], writes=[kx])
        P.op('act', lambda e: e.activation(out=sq[:], in_=xt[b][:], func=AF.Square, accum_out=ssq[b][:]), reads=[kx], writes=['sq', ('ssq', b)])
        P.op('act', lambda e: e.activation(out=ssq[b][:], in_=ssq[b][:], func=AF.Sqrt, scale=1.0 / D, bias=self.epsc[:, 0:1]), reads=[('ssq', b), 'epsc'], writes=[('ssq', b)])
        P.op('dve', lambda e: e.reciprocal(out=rstd[b][:], in_=ssq[b][:]), reads=[('ssq', b)], writes=[('rstd', b)])
        P.op('dve', lambda e: e.scalar_tensor_tensor(out=sq[:], in0=xt[b][:], scalar=rstd[b][:, 0:1], in1=A[:], op0=ALU.mult, op1=ALU.mult), reads=[kx, ('rstd', b), kA, 'sq'], writes=['sq'])
        P.op('pool', lambda e: e.tensor_tensor(out=hb[b][:], in0=sq[:], in1=B[:], op=ALU.add), reads=['sq', kB], writes=[('hb', b)])
        for q4 in range(4):
            pb = 6 + (q4 % 2)
            pst = self.pst[q4 % 2]
            for kk in range(4):
                k = q4 * 4 + kk
                P.op('pe', (lambda k=k, kk=kk, pst=pst: lambda e: e.transpose(pst[:, kk * 128:(kk + 1) * 128], hb[b][:, k * 128:(k + 1) * 128], self.cstb[:, 0, :]))(),
                     reads=[('hb', b), 'cstb'], writes=[('ps', pb)])
            eng = 'act' if q4 % 2 == 0 else 'dve'
            if eng == 'act':
                P.op('act', (lambda q4=q4, pst=pst: lambda e: e.activation(out=hT[:, q4 * 4:q4 * 4 + 4, col:col + 128], in_=pst[:, 0:512].rearrange("p (a b) -> p a b", a=4), func=AF.Copy))(),
                     reads=[('ps', pb)], writes=[hT_key])
            else:
                P.op('dve', (lambda q4=q4, pst=pst: lambda e: e.tensor_copy(out=hT[:, q4 * 4:q4 * 4 + 4, col:col + 128], in_=pst[:, 0:512].rearrange("p (a b) -> p a b", a=4)))(),
                     reads=[('ps', pb)], writes=[hT_key])

    def ln_bufs(self, T):
        return dict(xt=[T("xt", [128, D]) for _ in range(2)], sq=T("sq", [128, D]),
                    hb=[T("hb", [128, D], BF16) for _ in range(2)],
                    ssq=[T("ssq", [128, 1]) for _ in range(2)], rstd=[T("rstd", [128, 1]) for _ in range(2)], n=0)

    def ffn_phase(self, i):
        nc, P = self.nc, self.P
        last = (i == 3)
        k1 = self.convert_weight(self.wb_1, self.w1[i], D, DFF, 'wb1')
        k3 = self.convert_weight(self.wb_3, self.w3[i], D, DFF, 'wb3')
        k2 = self.convert_weight(self.wb_2, self.w2[i], DFF, D, 'wb2')
        w1v = self.wb_1.rearrange("(k p) n -> p k n", p=128)
        w3v = self.wb_3.rearrange("(k p) n -> p k n", p=128)
        w2v = self.wb_2.rearrange("(k p) n -> p k n", p=128)
        with contextlib.ExitStack() as st:
            T = lambda n, s, d=F32: st.enter_context(nc.sbuf_tensor(self.nm(n), s, d))
            vec = self.alloc_vec(T)
            bufs = self.ln_bufs(T)
            hT = T("hT", [128, 16, 512], BF16)
            gT = T("gT", [128, 44, 512], BF16)
            wa = [T("wa", [128, 16, 512], BF16) for _ in range(2)]
            wc = [T("wc", [128, 11, 512], BF16) for _ in range(2)]
            sa = [T("sa", [128, 512]) for _ in range(2)]
            tmp = [T("tmp", [128, 512]) for _ in range(2)]
            xo = [T("xo", [128, 512]) for _ in range(4)]
            nwa = 0
            nwc = 0
            nsa = 0
            nxo = 0
            for (t0, ntl) in groups_of_tiles(include_ctx=not last):
                W = ntl * 128
                self.load_vec(vec, i, 1, 1 if t0 >= 32 else 0)
                for tt in range(ntl):
                    self.ln_tile(T, bufs, t0 + tt, vec, hT, 'hT', tt * 128)
                for fb in range(11):
                    tiles = []
                    for (wv, kk) in ((w1v, k1), (w3v, k3)):
                        b = nwa % 2
                        nwa += 1
                        self.dma(wa[b][:], wv[:, :, fb * 512:(fb + 1) * 512], reads=kk, writes=[('wa', b)])
                        tiles.append(b)
                    for fc in range(4):
                        pa, pb = 0 + (fc % 2) * 2, 1 + (fc % 2) * 2
                        for (pp, b) in ((pa, tiles[0]), (pb, tiles[1])):
                            for k in range(16):
                                P.op('pe', (lambda pp=pp, b=b, k=k, fc=fc: lambda e: e.matmul(self.ps[pp][:, 0:W], wa[b][:, k, fc * 128:(fc + 1) * 128], hT[:, k, 0:W], start=(k == 0), stop=(k == 15)))(),
                                     reads=[('wa', b), 'hT'], writes=[('ps', pp)])
                        sb = nsa % 2
                        nsa += 1
                        P.op('act', (lambda pa=pa, sb=sb: lambda e: e.activation(out=sa[sb][:, 0:W], in_=self.ps[pa][:, 0:W], func=AF.Silu))(),
                             reads=[('ps', pa)], writes=[('sa', sb)])
                        P.op('dve', (lambda pb=pb, sb=sb, f=fb * 4 + fc: lambda e: e.tensor_tensor(out=gT[:, f, 0:W], in0=sa[sb][:, 0:W], in1=self.ps[pb][:, 0:W], op=ALU.mult))(),
                             reads=[('ps', pb), ('sa', sb)], writes=[('gT', fb * 4 + fc)])
                Gt, kG = vec['G'], 'v_G'
                for dc in range(4):
                    for kp in range(4):
                        b = nwc % 2
                        nwc += 1
                        self.dma(wc[b][:], w2v[:, kp * 11:(kp + 1) * 11, dc * 512:(dc + 1) * 512], reads=k2, writes=[('wc', b)])
                        for tt in range(ntl):
                            for k in range(11):
                                f = kp * 11 + k
                                P.op('pe', (lambda tt=tt, b=b, k=k, f=f: lambda e: e.matmul(self.ps[tt][:, :], gT[:, f, tt * 128:(tt + 1) * 128], wc[b][:, k, :], start=(f == 0), stop=(f == 43)))(),
                                     reads=[('wc', b), ('gT', f)], writes=[('ps', tt)])
                    for tt in range(ntl):
                        t = t0 + tt
                        b = nxo % 4
                        nxo += 1
                        self.dma(xo[b][:], self.xs[t * 128:(t + 1) * 128, dc * 512:(dc + 1) * 512], reads=[('xs', t)], writes=[('xo', b)])
                        tb = b % 2
                        P.op('dve', (lambda tt=tt, tb=tb, dc=dc: lambda e: e.tensor_tensor(out=tmp[tb][:], in0=self.ps[tt][:, :], in1=Gt[:, dc * 512:(dc + 1) * 512], op=ALU.mult))(),
                             reads=[('ps', tt), kG], writes=[('tmp', tb)])
                        P.op('pool', (lambda b=b, tb=tb: lambda e: e.tensor_tensor(out=xo[b][:], in0=xo[b][:], in1=tmp[tb][:], op=ALU.add))(),
                             reads=[('tmp', tb), ('xo', b)], writes=[('xo', b)])
                        self.dma(self.xs[t * 128:(t + 1) * 128, dc * 512:(dc + 1) * 512], xo[b][:], reads=[('xo', b)], writes=[('xs', t)])
        P.barrier()

    def na_layer(self, i, j):
        nc, P = self.nc, self.P
        if self.only:
            kw, ko = [], []
        else:
            kw = self.convert_weight(self.wb_a, self.na_wqkv[j], D, 3 * D, 'wba')
            ko = self.convert_weight(self.wb_o, self.na_wo[j], D, D, 'wbo')
        wv = self.wb_a.rearrange("(k p) n -> p k n", p=128)
        if self.only:
            with contextlib.ExitStack() as st:
                zt_ = st.enter_context(nc.sbuf_tensor(self.nm("zt"), [128, NT], BF16))
                P.op('pool', lambda e: e.memset(zt_[:], 0.25), writes=['zt_'])
                for h in range(16):
                    self.dma(self.KT[h], zt_[:], reads=['zt_'])
                    self.dma(self.QT[h], zt_[:], reads=['zt_'])
                Vv_ = self.V.rearrange("(t p) d -> p t d", p=128)
                for t in range(NTILE):
                    self.dma(Vv_[:, t, :], zt_[:, 0:D], reads=['zt_'])
                P.barrier()
        with contextlib.ExitStack() as st:
          if not self.only:
            T = lambda n, s, d=F32: st.enter_context(nc.sbuf_tensor(self.nm(n), s, d))
            vec = self.alloc_vec(T)
            bufs = self.ln_bufs(T)
            hT = T("hT", [128, 16, 512], BF16)
            wa = [T("wa", [128, 16, 512], BF16) for _ in range(2)]
            gq = T("gq", [128, 1]); gk = T("gk", [128, 1])
            self.dma(gq[:], self.na_qg[j], writes=['gq'])
            self.dma(gk[:], self.na_kg[j], writes=['gk'])
            P.op('dve', lambda e: e.tensor_scalar(out=gq[:], in0=gq[:], scalar1=128 ** -0.5, scalar2=None, op0=ALU.mult), reads=['gq'], writes=['gq'])
            sq = [T("qsq", [128, 512]) for _ in range(2)]
            rs = [T("qrs", [128, 512]) for _ in range(2)]
            qo = [T("qo", [128, 512], BF16) for _ in range(2)]
            vo = [T("vo", [128, 512], BF16) for _ in range(2)]
            nwa = 0
            nq = 0
            nv = 0
            for (t0, ntl) in groups_of_tiles():
                W = ntl * 128
                c0 = t0 * 128
                self.load_vec(vec, i, 0, 1 if t0 >= 32 else 0)
                for tt in range(ntl):
                    self.ln_tile(T, bufs, t0 + tt, vec, hT, 'hT', tt * 128)
                for cb in range(12):
                    b = nwa % 2
                    nwa += 1
                    self.dma(wa[b][:], wv[:, :, cb * 512:(cb + 1) * 512], reads=kw, writes=[('wa', b)])
                    if cb < 8:
                        for hh in range(4):
                            h = (cb % 4) * 4 + hh
                            pp = hh % 2
                            for k in range(16):
                                P.op('pe', (lambda pp=pp, b=b, k=k, hh=hh: lambda e: e.matmul(self.ps[pp][:, 0:W], wa[b][:, k, hh * 128:(hh + 1) * 128], hT[:, k, 0:W], start=(k == 0), stop=(k == 15)))(),
                                     reads=[('wa', b), 'hT'], writes=[('ps', pp)])
                            qb = nq % 2
                            nq += 1
                            P.op('act', (lambda pp=pp, qb=qb: lambda e: e.activation(out=sq[qb][:, 0:W], in_=self.ps[pp][:, 0:W], func=AF.Square))(),
                                 reads=[('ps', pp)], writes=[('qsq', qb)])
                            p2 = 2 + pp
                            P.op('pe', (lambda p2=p2, qb=qb: lambda e: e.matmul(self.ps[p2][:, 0:W], self.cst[:, 1, :], sq[qb][:, 0:W], start=True, stop=True))(),
                                 reads=['cst', ('qsq', qb)], writes=[('ps', p2)])
                            P.op('act', (lambda p2=p2, qb=qb: lambda e: e.activation(out=rs[qb][:, 0:W], in_=self.ps[p2][:, 0:W], func=AF.Sqrt, scale=1.0 / 128, bias=self.epsc[:, 0:1]))(),
                                 reads=[('ps', p2), 'epsc'], writes=[('qrs', qb)])
                            P.op('dve', (lambda qb=qb: lambda e: e.reciprocal(out=rs[qb][:, 0:W], in_=rs[qb][:, 0:W]))(),
                                 reads=[('qrs', qb)], writes=[('qrs', qb)])
                            gg, gkey = (gq, 'gq') if cb < 4 else (gk, 'gk')
                            P.op('dve', (lambda pp=pp, qb=qb, gg=gg: lambda e: e.scalar_tensor_tensor(out=qo[qb][:, 0:W], in0=self.ps[pp][:, 0:W], scalar=gg[:, 0:1], in1=rs[qb][:, 0:W], op0=ALU.mult, op1=ALU.mult))(),
                                 reads=[('ps', pp), ('qrs', qb), gkey], writes=[('qo', qb)])
                            dst = self.QT if cb < 4 else self.KT
                            self.dma(dst[h, :, c0:c0 + W], qo[qb][:, 0:W], reads=[('qo', qb)], writes=[('QK', cb < 4, h, t0)])
                    else:
                        vc = cb - 8
                        for tt in range(ntl):
                            pp = 4 + tt % 2
                            for k in range(16):
                                P.op('pe', (lambda pp=pp, b=b, k=k, tt=tt: lambda e: e.matmul(self.ps[pp][:, :], hT[:, k, tt * 128:(tt + 1) * 128], wa[b][:, k, :], start=(k == 0), stop=(k == 15)))(),
                                     reads=[('wa', b), 'hT'], writes=[('ps', pp)])
                            vb = nv % 2
                            nv += 1
                            P.op('act', (lambda pp=pp, vb=vb: lambda e: e.activation(out=vo[vb][:], in_=self.ps[pp][:, :], func=AF.Copy))(),
                                 reads=[('ps', pp)], writes=[('vo', vb)])
                            t = t0 + tt
                            self.dma(self.V[t * 128:(t + 1) * 128, vc * 512:(vc + 1) * 512], vo[vb][:], reads=[('vo', vb)], writes=[('V', t, vc)])
        P.barrier()
        if self.stop < 3:
            return
        with contextlib.ExitStack() as st:
            T = lambda n, s, d=F32: st.enter_context(nc.sbuf_tensor(self.nm(n), s, d))
            NHB = int(os.environ.get('MK_NHB', '2'))
            kt = [T("kt", [128, NT], BF16) for _ in range(NHB)]
            qt = [T("qt", [128, NT], BF16) for _ in range(NHB)]
            vt = [T("vt", [128, NTILE, 128], BF16) for _ in range(NHB)]
            bf = [T("bf", [128, 21, 128]) for _ in range(NHB)]
            bh = [T("bh", [128, 21, 128], BF16) for _ in range(NHB)]
            bl = [T("bl", [128, 21, 128], BF16) for _ in range(NHB)]
            ot = [T("ot", [128, NT], BF16) for _ in range(NHB)]
            pt = [T("pt", [128, 1024], BF16) for _ in range(3)]
            rd = [T("rd", [128, 256]) for _ in range(2)]
            npt = 0
            nrd = 0
            Vv = self.V.rearrange("(t p) d -> p t d", p=128)
            ident = self.cstb[:, 0, :]
            ones = self.cstb[:, 1, :]
            P2H = int(os.environ.get('MK_P2H', '16')); P2W = int(os.environ.get('MK_P2W', '99')); P2X = os.environ.get('MK_P2X', '')
            for h in range(P2H):
                hb_ = h % NHB
                self.dma(kt[hb_][:], self.KT[h], writes=[('kt', hb_)])
                self.dma(qt[hb_][:], self.QT[h], writes=[('qt', hb_)])
                self.dma(vt[hb_][:], Vv[:, :, h * 128:(h + 1) * 128], writes=[('vt', hb_)])
                self.dma(bf[hb_][:], self.na_bias[j][h], writes=[('bf', hb_)])
                P.op('act', (lambda hb_=hb_: lambda e: e.activation(out=bh[hb_][:], in_=bf[hb_][:], func=AF.Copy))(), reads=[('bf', hb_)], writes=[('bh', hb_)])
                P.op('pool', (lambda hb_=hb_: lambda e: e.tensor_tensor(out=bf[hb_][:], in0=bf[hb_][:], in1=bh[hb_][:], op=ALU.subtract))(), reads=[('bf', hb_), ('bh', hb_)], writes=[('bf', hb_)])
                P.op('pool', (lambda hb_=hb_: lambda e: e.tensor_copy(out=bl[hb_][:], in_=bf[hb_][:]))(), reads=[('bf', hb_)], writes=[('bl', hb_)])
                work = []
                for rp in range(32):
                    if rp == 0:
                        blocks = [(kk, 5 + kk) for kk in range(4)]
                    elif rp == 1:
                        blocks = [(kk, 9 + kk) for kk in range(4)]
                    elif rp == 30:
                        blocks = [(28 + kk, 13 + kk) for kk in range(4)]
                    elif rp == 31:
                        blocks = [(28 + kk, 17 + kk) for kk in range(4)]
                    else:
                        blocks = [(rp - 2 + kk, kk) for kk in range(5)]
                    blocks += [(32, None), (33, None)]
                    work.append((rp * 128, 128, blocks))
                work.append((SEQ, 128, [(32, None), (33, None)]))
                work.append((SEQ + 128, 128, [(32, None), (33, None)]))
                for wi, (q0, qw, blocks) in enumerate(work[int(os.environ.get("MK_P2S", "0")):P2W]):
                    sset = wi % 2
                    pS = [self.ps[sset * 2], self.ps[sset * 2 + 1]]
                    pO = self.ps[4 + sset]
                    pb_ = npt % 3
                    npt += 1
                    nb = len(blocks)
                    per_bank = 512 // qw
                    for bi, (ktile, bidx) in enumerate(blocks):
                        bank = bi // per_bank
                        off = (bi % per_bank) * qw
                        o_ap = pS[bank][:, off:off + qw]
                        kp = ('ps', sset * 2 + bank)
                        if bidx is not None:
                            P.op('pe', (lambda o_ap=o_ap, bidx=bidx, hb_=hb_: lambda e: e.matmul(o_ap, ident, bh[hb_][:, bidx, :], start=True, stop=False))(), reads=['cstb', ('bh', hb_)], writes=[kp])
                            P.op('pe', (lambda o_ap=o_ap, bidx=bidx, hb_=hb_: lambda e: e.matmul(o_ap, ident, bl[hb_][:, bidx, :], start=False, stop=False))(), reads=[('bl', hb_)], writes=[kp])
                        P.op('pe', (lambda o_ap=o_ap, ktile=ktile, hb_=hb_, q0=q0, qw=qw, st_=(bidx is None): lambda e: e.matmul(o_ap, kt[hb_][:, ktile * 128:(ktile + 1) * 128], qt[hb_][:, q0:q0 + qw], start=st_, stop=True))(),
                             reads=[('kt', hb_), ('qt', hb_)], writes=[kp])
                    nbank = (nb + per_bank - 1) // per_bank
                    X = P2X if qw == 256 else ''
                    if 'a' in X:
                        continue
                    for bank in range(nbank):
                        nblk = min(per_bank, nb - bank * per_bank)
                        Wd = nblk * qw
                        P.op('act', (lambda bank=bank, Wd=Wd, pb_=pb_: lambda e: e.activation(out=pt[pb_][:, bank * 512:bank * 512 + Wd], in_=pS[bank][:, 0:Wd], func=AF.Exp))(),
                             reads=[('ps', sset * 2 + bank)], writes=[('pt', pb_, bank)])
                    if 'b' in X:
                        continue
                    for bi, (ktile, bidx) in enumerate(blocks):
                        bank = bi // per_bank
                        off = bank * 512 + (bi % per_bank) * qw
                        P.op('pe', (lambda ktile=ktile, off=off, bi=bi, nb=nb, hb_=hb_, pb_=pb_, qw=qw: lambda e: e.matmul(pO[:, 0:qw], vt[hb_][:, ktile, :], pt[pb_][:, off:off + qw], start=(bi == 0), stop=(bi == nb - 1)))(),
                             reads=[('vt', hb_), ('pt', pb_, bank)], writes=[('ps', 4 + sset)])
                    for bi, (ktile, bidx) in enumerate(blocks):
                        bank = bi // per_bank
                        off = bank * 512 + (bi % per_bank) * qw
                        P.op('pe', (lambda off=off, bi=bi, nb=nb, pb_=pb_, qw=qw: lambda e: e.matmul(pO[:, 256:256 + qw], ones, pt[pb_][:, off:off + qw], start=(bi == 0), stop=(bi == nb - 1)))(),
                             reads=['cstb', ('pt', pb_, bank)], writes=[('ps', 4 + sset)])
                    if 'c' in X:
                        continue
                    rb_ = nrd % 2
                    nrd += 1
                    P.op('dve', (lambda rb_=rb_, qw=qw: lambda e: e.reciprocal(out=rd[rb_][:, 0:qw], in_=pO[:, 256:256 + qw]))(), reads=[('ps', 4 + sset)], writes=[('rd', rb_)])
                    P.op('dve', (lambda rb_=rb_, qw=qw, q0=q0, hb_=hb_: lambda e: e.tensor_tensor(out=ot[hb_][:, q0:q0 + qw], in0=pO[:, 0:qw], in1=rd[rb_][:, 0:qw], op=ALU.mult))(),
                         reads=[('ps', 4 + sset), ('rd', rb_)], writes=[('ot', hb_)])
                self.dma(self.OT[h], ot[hb_][:], reads=[('ot', hb_)], writes=[('OT', h)])
        P.barrier()
        if self.stop < 4 or self.only == 'p2':
            return
        self.out_proj(i, self.OT, 16, ko, 'a')

    def out_proj(self, i, YT, nh, ko, key):
        nc, P = self.nc, self.P
        last = (i == 3)
        with contextlib.ExitStack() as st:
            T = lambda n, s, d=F32: st.enter_context(nc.sbuf_tensor(self.nm(n), s, d))
            vec = self.alloc_vec(T)
            wo = [T("wo", [128, nh, 512], BF16) for _ in range(2)]
            yt = T("yt", [128, nh, 512], BF16)
            xo = [T("xo", [128, D]) for _ in range(4)]
            tmp = [T("tmp", [128, 512]) for _ in range(2)]
            wov = self.wb_o.rearrange("(k p) n -> p k n", p=128)
            YTv = YT.rearrange("h p t -> p h t")
            nwo = 0
            ntmp = 0
            for (t0, ntl) in groups_of_tiles(include_ctx=not last):
                W = ntl * 128
                self.load_vec(vec, i, 0, 1 if t0 >= 32 else 0)
                Gt, kG = vec['G'], 'v_G'
                self.dma(yt[:, :, 0:W], YTv[:, :, t0 * 128:t0 * 128 + W], writes=['yt'])
                for tt in range(ntl):
                    t = t0 + tt
                    self.dma(xo[tt][:], self.xs[t * 128:(t + 1) * 128, :], reads=[('xs', t)], writes=[('xo', tt)])
                for dc in range(4):
                    b = nwo % 2
                    nwo += 1
                    self.dma(wo[b][:], wov[:, 0:nh, dc * 512:(dc + 1) * 512], reads=ko, writes=[('wo', b)])
                    for tt in range(ntl):
                        pp = (dc * 4 + tt) % 6
                        for k in range(nh):
                            P.op('pe', (lambda pp=pp, b=b, k=k, tt=tt: lambda e: e.matmul(self.ps[pp][:, :], yt[:, k, tt * 128:(tt + 1) * 128], wo[b][:, k, :], start=(k == 0), stop=(k == nh - 1)))(),
                                 reads=[('wo', b), 'yt'], writes=[('ps', pp)])
                        tb = ntmp % 2
                        ntmp += 1
                        P.op('dve', (lambda pp=pp, tb=tb, dc=dc: lambda e: e.tensor_tensor(out=tmp[tb][:], in0=self.ps[pp][:, :], in1=Gt[:, dc * 512:(dc + 1) * 512], op=ALU.mult))(),
                             reads=[('ps', pp), kG], writes=[('tmp', tb)])
                        P.op('pool', (lambda tt=tt, tb=tb, dc=dc: lambda e: e.tensor_tensor(out=xo[tt][:, dc * 512:(dc + 1) * 512], in0=xo[tt][:, dc * 512:(dc + 1) * 512], in1=tmp[tb][:], op=ALU.add))(),
                             reads=[('tmp', tb), ('xo', tt)], writes=[('xo', tt)])
                for tt in range(ntl):
                    t = t0 + tt
                    self.dma(self.xs[t * 128:(t + 1) * 128, :], xo[tt][:], reads=[('xo', tt)], writes=[('xs', t)])
        P.barrier()

    def gdn_layer(self, i, j):
        nc, P = self.nc, self.P
        only = self.only
        GN = int(os.environ.get('MK_GN', '999'))
        if only:
            kw, ko, kba = [], [], []
        else:
            kw = self.convert_weight(self.wb_a, self.g_win[j], D, 12288, 'wba')
            ko = self.convert_weight(self.wb_o, self.g_wo[j], 4096, D, 'wbo')
            kba = self.convert_weight(self.wb_ba, self.g_wba[j], D, 128, 'wbba')
        wv = self.wb_a.rearrange("(k p) n -> p k n", p=128)
        ident32 = self.cst[0:32, 0, 0:32]
        pc = lambda tk: tk + 1 if tk < SEQ else tk + 4
        with contextlib.ExitStack() as st:
          if only in ('', 'g1'):
            T = lambda n, s, d=F32: st.enter_context(nc.sbuf_tensor(self.nm(n), s, d))
            vec = self.alloc_vec(T)
            bufs = self.ln_bufs(T)
            hT = T("hT", [128, 16, 512], BF16)
            wa = [T("wa", [128, 16, 512], BF16) for _ in range(2)]
            wba = T("wba", [128, 16, 128], BF16)
            self.dma(wba[:], self.wb_ba.rearrange("(k p) n -> p k n", p=128), reads=kba, writes=['wba_t'])
            dtb = T("dtb", [128, 64]); nega = T("nega", [128, 64])
            self.dma(dtb[:], self.g_dtb[j:j + 1, :].partition_broadcast(128), writes=['dtb'])
            self.dma(nega[:], self.g_alog[j:j + 1, :].partition_broadcast(128), writes=['nega'])
            P.op('act', lambda e: e.activation(out=nega[:], in_=nega[:], func=AF.Exp), reads=['nega'], writes=['nega'])
            P.op('dve', lambda e: e.tensor_scalar(out=nega[:], in0=nega[:], scalar1=-1.0, scalar2=None, op0=ALU.mult), reads=['nega'], writes=['nega'])
            stg = [T("stg", [128, 512]) for _ in range(3)]
            gst_ = [T("gst", [128, 64]) for _ in range(2)]
            bst_ = [T("bst", [128, 64]) for _ in range(2)]
            xa = T("xa", [128, 64])
            nwa = 0
            nst = 0
            ngb = 0
            for (t0, ntl) in groups_of_tiles()[-GN:]:
                W = ntl * 128
                self.load_vec(vec, i, 0, 1 if t0 >= 32 else 0)
                for tt in range(ntl):
                    self.ln_tile(T, bufs, t0 + tt, vec, hT, 'hT', tt * 128)
                pc0 = pc(t0 * 128)
                for cb in range(24):
                    b = nwa % 2
                    nwa += 1
                    self.dma(wa[b][:], wv[:, :, cb * 512:(cb + 1) * 512], reads=kw, writes=[('wa', b)])
                    if cb < 16:
                        for cc in range(4):
                            c = cb * 4 + cc
                            pp = cc % 2
                            for k in range(16):
                                P.op('pe', (lambda pp=pp, b=b, k=k, cc=cc: lambda e: e.matmul(self.ps[pp][:, 0:W], wa[b][:, k, cc * 128:(cc + 1) * 128], hT[:, k, 0:W], start=(k == 0), stop=(k == 15)))(),
                                     reads=[('wa', b), 'hT'], writes=[('ps', pp)])
                            sb = nst % 3
                            nst += 1
                            P.op('act', (lambda pp=pp, sb=sb: lambda e: e.activation(out=stg[sb][:, 0:W], in_=self.ps[pp][:, 0:W], func=AF.Copy))(),
                                 reads=[('ps', pp)], writes=[('stg', sb)])
                            self.dma(self.PRE[c, :, pc0:pc0 + W], stg[sb][:, 0:W], reads=[('stg', sb)], writes=[('PRE', c, t0)])
                    else:
                        zc = cb - 16
                        for tt in range(ntl):
                            pp = 2 + tt % 2
                            for k in range(16):
                                P.op('pe', (lambda pp=pp, b=b, k=k, tt=tt: lambda e: e.matmul(self.ps[pp][:, :], hT[:, k, tt * 128:(tt + 1) * 128], wa[b][:, k, :], start=(k == 0), stop=(k == 15)))(),
                                     reads=[('wa', b), 'hT'], writes=[('ps', pp)])
                            sb = nst % 3
                            nst += 1
                            P.op('act', (lambda pp=pp, sb=sb: lambda e: e.activation(out=stg[sb][:], in_=self.ps[pp][:, :], func=AF.Silu))(),
                                 reads=[('ps', pp)], writes=[('stg', sb)])
                            t = t0 + tt
                            self.dma(self.ZTM[t * 128:(t + 1) * 128, zc * 512:(zc + 1) * 512], stg[sb][:], reads=[('stg', sb)], writes=[('ZTM', t, zc)])
                for tt in range(ntl):
                    t = t0 + tt
                    pp = 4
                    for k in range(16):
                        P.op('pe', (lambda k=k, tt=tt: lambda e: e.matmul(self.ps[4][:, 0:128], hT[:, k, tt * 128:(tt + 1) * 128], wba[:, k, :], start=(k == 0), stop=(k == 15)))(),
                             reads=['wba_t', 'hT'], writes=[('ps', 4)])
                    gb = ngb % 2
                    ngb += 1
                    for d in range(2):
                        P.op('act', (lambda d=d, gb=gb: lambda e: e.activation(out=bst_[gb][:, d * 32:(d + 1) * 32], in_=self.ps[4][:, d * 64:d * 64 + 32], func=AF.Sigmoid))(),
                             reads=[('ps', 4)], writes=[('bst', gb)])
                        P.op('dve', (lambda d=d: lambda e: e.tensor_tensor(out=xa[:, d * 32:(d + 1) * 32], in0=self.ps[4][:, d * 64 + 32:d * 64 + 64], in1=dtb[:, d * 32:(d + 1) * 32], op=ALU.add))(),
                             reads=[('ps', 4), 'dtb'], writes=['xa'])
                    P.op('act', lambda e: e.activation(out=xa[:], in_=xa[:], func=AF.Exp), reads=['xa'], writes=['xa'])
                    P.op('act', lambda e: e.activation(out=xa[:], in_=xa[:], func=AF.Ln, bias=self.cst[:, 1, 0:1]), reads=['xa', 'cst'], writes=['xa'])
                    P.op('dve', (lambda gb=gb: lambda e: e.tensor_tensor(out=gst_[gb][:], in0=xa[:], in1=nega[:], op=ALU.mult))(), reads=['xa', 'nega'], writes=[('gst', gb)])
                    self.dma(self.G[t * 128:(t + 1) * 128, :], gst_[gb][:], reads=[('gst', gb)], writes=[('G', t)])
                    self.dma(self.BETA[t * 128:(t + 1) * 128, :], bst_[gb][:], reads=[('bst', gb)], writes=[('BETA', t)])
        P.barrier()
        with contextlib.ExitStack() as st:
          if only in ('', 'g1b'):
            T = lambda n, s, d=F32: st.enter_context(nc.sbuf_tensor(self.nm(n), s, d))
            NP = 4360
            pre = [T("pre", [128, NP]) for _ in range(2)]
            acc = T("acc", [128, NT]); sl = T("sl", [128, NT]); sq = T("sq2", [128, NT]); qn = T("qn", [128, NT])
            rs = [T("rs", [128, 512]) for _ in range(2)]
            tst = [T("tst", [128, 4, 128]) for _ in range(2)]
            cw = T("cw", [128, 64, 4])
            self.dma(cw[:], self.g_conv[j], writes=['cw'])
            nrs = 0
            nts = 0
            KTMv = self.KTM.rearrange("(a p) d -> p a d", p=128)
            VTMv = self.VTM.rearrange("(a p) d -> p a d", p=128)
            pieces = [(c0, 512) for c0 in range(0, SEQ, 512)] + [(SEQ, 256)]
            for c in ([0, 1, 16, 17, 32, 33] if GN < 999 else range(64)):
                b = c % 2
                kp = ('pre', b)
                self.dma(pre[b][:], self.PRE[c], writes=[kp])
                P.op('pool', (lambda b=b: lambda e: e.memset(pre[b][:, 0:1], 0.0))(), reads=[], writes=[kp])
                P.op('pool', (lambda b=b: lambda e: e.memset(pre[b][:, 4097:4100], 0.0))(), reads=[], writes=[kp])
                P.op('pool', (lambda b=b: lambda e: e.memset(pre[b][:, 4356:4360], 0.0))(), reads=[], writes=[kp])
                for (o0, i0, Wd) in ((0, 0, SEQ), (SEQ, 4099, CTX)):
                    P.op('dve', (lambda b=b, o0=o0, i0=i0, Wd=Wd, c=c: lambda e: e.tensor_scalar(out=acc[:, o0:o0 + Wd], in0=pre[b][:, i0:i0 + Wd], scalar1=cw[:, c, 0:1], scalar2=None, op0=ALU.mult))(),
                         reads=[kp, 'cw'], writes=['acc'])
                    for m in range(1, 4):
                        P.op('dve', (lambda b=b, o0=o0, i0=i0, Wd=Wd, c=c, m=m: lambda e: e.scalar_tensor_tensor(out=acc[:, o0:o0 + Wd], in0=pre[b][:, i0 + m:i0 + m + Wd], scalar=cw[:, c, m:m + 1], in1=acc[:, o0:o0 + Wd], op0=ALU.mult, op1=ALU.add))(),
                             reads=[kp, 'cw', 'acc'], writes=['acc'])
                P.op('act', lambda e: e.activation(out=sl[:], in_=acc[:], func=AF.Silu), reads=['acc'], writes=['sl'])
                src, ksrc = sl, 'sl'
                if c < 32:
                    P.op('act', lambda e: e.activation(out=sq[:], in_=sl[:], func=AF.Square), reads=['sl'], writes=['sq2'])
                    for pi, (c0, Wd) in enumerate(pieces):
                        pp = pi % 2
                        rb = nrs % 2
                        nrs += 1
                        P.op('pe', (lambda pp=pp, c0=c0, Wd=Wd: lambda e: e.matmul(self.ps[pp][:, 0:Wd], self.cst[:, 1, :], sq[:, c0:c0 + Wd], start=True, stop=True))(),
                             reads=['cst', 'sq2'], writes=[('ps', pp)])
                        P.op('act', (lambda pp=pp, rb=rb, Wd=Wd: lambda e: e.activation(out=rs[rb][:, 0:Wd], in_=self.ps[pp][:, 0:Wd], func=AF.Sqrt, bias=self.epsc[:, 0:1]))(),
                             reads=[('ps', pp), 'epsc'], writes=[('rs', rb)])
                        P.op('dve', (lambda rb=rb, Wd=Wd: lambda e: e.reciprocal(out=rs[rb][:, 0:Wd], in_=rs[rb][:, 0:Wd]))(), reads=[('rs', rb)], writes=[('rs', rb)])
                        scl = (128 ** -0.5) if c < 16 else 1.0
                        P.op('dve', (lambda rb=rb, c0=c0, Wd=Wd, scl=scl: lambda e: e.scalar_tensor_tensor(out=qn[:, c0:c0 + Wd], in0=sl[:, c0:c0 + Wd], scalar=scl, in1=rs[rb][:, 0:Wd], op0=ALU.mult, op1=ALU.mult))(),
                             reads=['sl', ('rs', rb)], writes=['qn'])
                    dst = self.QTg if c < 16 else self.KTg
                    self.dma(dst[c % 16], qn[:], reads=['qn'], writes=[('QKg', c)])
                    src, ksrc = qn, 'qn'
                if c >= 16:
                    for g0 in range(0, NTILE, 4):
                        ng = min(4, NTILE - g0)
                        pp = 2 + (g0 // 4) % 2
                        for a in range(ng):
                            t = g0 + a
                            P.op('pe', (lambda pp=pp, a=a, t=t, src=src: lambda e: e.transpose(self.ps[pp][:, a * 128:(a + 1) * 128], src[:, t * 128:(t + 1) * 128], self.cst[:, 0, :]))(),
                                 reads=[ksrc, 'cst'], writes=[('ps', pp)])
                        tb = nts % 2
                        nts += 1
                        P.op('act', (lambda pp=pp, tb=tb, ng=ng: lambda e: e.activation(out=tst[tb][:, 0:ng, :], in_=self.ps[pp][:, 0:ng * 128].rearrange("p (a b) -> p a b", a=ng), func=AF.Copy))(),
                             reads=[('ps', pp)], writes=[('tst', tb)])
                        if c < 32:
                            dv = KTMv[:, g0:g0 + ng, (c - 16) * 128:(c - 15) * 128]
                        else:
                            dv = VTMv[:, g0:g0 + ng, (c - 32) * 128:(c - 31) * 128]
                        self.dma(dv, tst[tb][:, 0:ng, :], reads=[('tst', tb)], writes=[('TM', c, g0)])
        P.barrier()
        if self.stop < 3:
            return
        if only == 'g2':
            with contextlib.ExitStack() as st:
                zt_ = st.enter_context(nc.sbuf_tensor(self.nm("zt"), [128, 4096], F32))
                P.op('pool', lambda e: e.memset(zt_[:], 0.125), writes=['zt_'])
                for h in range(16):
                    self.dma(self.KTg[h, :, 0:4096], zt_[:], reads=['zt_'])
                    self.dma(self.QTg[h, :, 0:4096], zt_[:], reads=['zt_'])
                    self.dma(self.KTg[h, :, 4096:NT], zt_[:, 0:256], reads=['zt_'])
                    self.dma(self.QTg[h, :, 4096:NT], zt_[:, 0:256], reads=['zt_'])
                for t in range(NTILE):
                    self.dma(self.KTM[t * 128:(t + 1) * 128, :], zt_[:, 0:2048], reads=['zt_'])
                    self.dma(self.VTM[t * 128:(t + 1) * 128, :], zt_[:, 0:4096], reads=['zt_'])
                    self.dma(self.G[t * 128:(t + 1) * 128, :], zt_[:, 0:64], reads=['zt_'])
                    self.dma(self.BETA[t * 128:(t + 1) * 128, :], zt_[:, 0:64], reads=['zt_'])
                P.barrier()
        if only in ('', 'g2'):
            self.gdn_scan(GN)
        P.barrier()
        if self.stop < 4 or only not in ('', 'g3'):
            return
        with contextlib.ExitStack() as st:
            T = lambda n, s, d=F32: st.enter_context(nc.sbuf_tensor(self.nm(n), s, d))
            of = [T("of", [128, 4096]) for _ in range(2)]
            ob = [T("ob", [128, 4096]) for _ in range(2)]
            zt = [T("zt", [128, 4096]) for _ in range(2)]
            sq = T("sq3", [128, 4096])
            yb = T("yb", [128, 4096], BF16)
            yT = [T("yT", [128, 32, 128], BF16) for _ in range(2)]
            ssq = T("ssq3", [128, 32]); rst = T("rst3", [128, 32])
            nwt = T("nwt", [128, 128])
            self.dma(nwt[:], self.g_nw[j:j + 1, :].partition_broadcast(128), writes=['nwt'])
            YTv = self.YTg.rearrange("h p t -> p h t")
            for t in range(min(NTILE, GN)):
                b = t % 2
                self.dma(of[b][:], self.OF[t * 128:(t + 1) * 128, :], writes=[('of', b)])
                self.dma(ob[b][:], self.OB[t * 128:(t + 1) * 128, :], writes=[('ob', b)])
                self.dma(zt[b][:], self.ZTM[t * 128:(t + 1) * 128, :], writes=[('zt', b)])
                P.op('pool', (lambda b=b: lambda e: e.tensor_tensor(out=of[b][:], in0=of[b][:], in1=ob[b][:], op=ALU.add))(), reads=[('of', b), ('ob', b)], writes=[('of', b)])
                P.op('act', (lambda b=b: lambda e: e.activation(out=sq[:], in_=of[b][:], func=AF.Square))(), reads=[('of', b)], writes=['sq3'])
                P.op('dve', lambda e: e.tensor_reduce(out=ssq[:], in_=sq[:].rearrange("p (h d) -> p h d", h=32), axis=AX.X, op=ALU.add), reads=['sq3'], writes=['ssq3'])
                P.op('act', lambda e: e.activation(out=ssq[:], in_=ssq[:], func=AF.Sqrt, scale=1.0 / 128, bias=self.epsc[:, 0:1]), reads=['ssq3', 'epsc'], writes=['ssq3'])
                P.op('dve', lambda e: e.reciprocal(out=rst[:], in_=ssq[:]), reads=['ssq3'], writes=['rst3'])
                o3 = lambda b=b: of[b][:].rearrange("p (h d) -> p h d", h=32)
                P.op('dve', (lambda b=b: lambda e: e.tensor_tensor(out=sq[:].rearrange("p (h d) -> p h d", h=32), in0=of[b][:].rearrange("p (h d) -> p h d", h=32), in1=rst[:].unsqueeze(2).to_broadcast([128, 32, 128]), op=ALU.mult))(),
                     reads=[('of', b), 'rst3', 'sq3'], writes=['sq3'])
                P.op('pool', lambda e: e.tensor_tensor(out=sq[:].rearrange("p (h d) -> p h d", h=32), in0=sq[:].rearrange("p (h d) -> p h d", h=32), in1=nwt[:].unsqueeze(1).to_broadcast([128, 32, 128]), op=ALU.mult),
                     reads=['sq3', 'nwt'], writes=['sq3'])
                P.op('dve', (lambda b=b: lambda e: e.tensor_tensor(out=yb[:], in0=sq[:], in1=zt[b][:], op=ALU.mult))(), reads=['sq3', ('zt', b)], writes=['yb'])
                for q8 in range(8):
                    pst = self.pst[q8 % 2]
                    for kk in range(4):
                        hh = q8 * 4 + kk
                        P.op('pe', (lambda hh=hh, kk=kk, pst=pst: lambda e: e.transpose(pst[:, kk * 128:(kk + 1) * 128], yb[:, hh * 128:(hh + 1) * 128], self.cstb[:, 0, :]))(),
                             reads=['yb', 'cstb'], writes=[('ps', 6 + q8 % 2)])
                    eng = 'act' if q8 % 2 == 0 else 'dve'
                    if eng == 'act':
                        P.op('act', (lambda q8=q8, pst=pst, b=b: lambda e: e.activation(out=yT[b][:, q8 * 4:q8 * 4 + 4, :], in_=pst[:, 0:512].rearrange("p (a b) -> p a b", a=4), func=AF.Copy))(),
                             reads=[('ps', 6 + q8 % 2)], writes=[('yT', b)])
                    else:
                        P.op('dve', (lambda q8=q8, pst=pst, b=b: lambda e: e.tensor_copy(out=yT[b][:, q8 * 4:q8 * 4 + 4, :], in_=pst[:, 0:512].rearrange("p (a b) -> p a b", a=4)))(),
                             reads=[('ps', 6 + q8 % 2)], writes=[('yT', b)])
                self.dma(YTv[:, :, t * 128:(t + 1) * 128], yT[b][:], reads=[('yT', b)], writes=[('YTg', t)])
        P.barrier()
        self.out_proj(i, self.YTg, 32, ko, 'g')

    def gdn_scan(self, GN=999):
        nc, P = self.nc, self.P
        C = 32
        NCH = NT // C
        fwd = list(range(128, 136)) + list(range(0, 128))
        bwd = list(range(135, 127, -1)) + list(range(127, -1, -1))
        order = [fwd, bwd]
        NSL = 5
        with contextlib.ExitStack() as st:
            T = lambda n, s, d=F32: st.enter_context(nc.sbuf_tensor(self.nm(n), s, d))
            S = T("S", [128, 64, 128])
            P.op('pool', lambda e: e.memset(S[:], 0.0), writes=[('S', u) for u in range(64)])
            kq = [[T("kq", [128, 16, 2, C]) for _ in range(2)] for _ in range(2)]
            ktm = [[T("ktm", [C, 2048]) for _ in range(2)] for _ in range(2)]
            vtm = [[T("vtm", [C, 4096]) for _ in range(2)] for _ in range(2)]
            gg = [[T("gg", [C, 32]) for _ in range(2)] for _ in range(2)]
            bb = [[T("bb", [C, 32]) for _ in range(2)] for _ in range(2)]
            ost = [T("ost", [C, 4096]) for _ in range(2)]
            gc = [T("gc", [C, 32]) for _ in range(2)]
            egc = [T("egc", [C, 32]) for _ in range(2)]
            negc = [T("negc", [C, 32]) for _ in range(2)]
            kdf = [T("kdf", [C, 32]) for _ in range(2)]
            egl = [T("egl", [128, 32]) for _ in range(2)]
            sm = lambda n: [T(n, [C, 32]) for _ in range(NSL)]
            gTri, Dm, Em, AT, EmS, Mp, Nn = sm("gTri"), sm("Dm"), sm("Em"), sm("AT"), sm("EmS"), sm("Mp"), sm("Nn")
            Rb = [[T("Rb", [C, 32]) for _ in range(2)] for _ in range(NSL)]
            PQ = [[T("PQ", [C, 64]) for _ in range(2)] for _ in range(NSL)]
            rhs0 = [T("rhs0", [C, 128]) for _ in range(NSL)]
            vn = [T("vn", [C, 128]) for _ in range(NSL)]
            t2 = [T("t2", [C, 128]) for _ in range(NSL)]
            kdec = [T("kdec", [C, 128]) for _ in range(NSL)]
            KTgv = self.KTg.rearrange("h p t -> p h t")
            QTgv = self.QTg.rearrange("h p t -> p h t")
            tri = [self.cst[0:C, 2, 0:C], self.cst[0:C, 3, 0:C]]
            negm = [self.cst[0:C, 4, 0:C], self.cst[0:C, 5, 0:C]]
            offd = self.cst[0:C, 6, 0:C]
            ident = self.cst[0:C, 0, 0:C]
            ones_c = self.cst[0:C, 1, :]
            psh = self.ps[5]
            slot_ctr = [0]
            for s_ in range(min(NCH, GN)):
                lb = s_ % 2
                for d in range(2):
                    ci = order[d][s_]
                    r0 = ci * C
                    self.dma(kq[d][lb][:, :, 0, :], KTgv[:, :, r0:r0 + C], writes=[('kq', d, lb, 0)])
                    self.dma(kq[d][lb][:, :, 1, :], QTgv[:, :, r0:r0 + C], writes=[('kq', d, lb, 1)])
                    self.dma(ktm[d][lb][:], self.KTM[r0:r0 + C, :], writes=[('ktm', d, lb)])
                    self.dma(vtm[d][lb][:], self.VTM[r0:r0 + C, :], writes=[('vtm', d, lb)])
                    self.dma(gg[d][lb][:], self.G[r0:r0 + C, d * 32:(d + 1) * 32], writes=[('gg', d, lb)])
                    self.dma(bb[d][lb][:], self.BETA[r0:r0 + C, d * 32:(d + 1) * 32], writes=[('bb', d, lb)])
                for d in range(2):
                    kg = ('gg', d, lb)
                    P.op('pe', (lambda d=d, lb=lb: lambda e: e.matmul(psh[0:C, d * 32:(d + 1) * 32], tri[d], gg[d][lb][:], start=True, stop=True))(), reads=['cst', kg], writes=[('ps', 5)])
                    P.op('pe', (lambda d=d, lb=lb: lambda e: e.matmul(psh[:, 64 + d * 32:64 + (d + 1) * 32], ones_c, gg[d][lb][:], start=True, stop=True))(), reads=['cst', kg], writes=[('ps', 5)])
                    P.op('act', (lambda d=d: lambda e: e.activation(out=gc[d][:], in_=psh[0:C, d * 32:(d + 1) * 32], func=AF.Copy))(), reads=[('ps', 5)], writes=[('gc', d)])
                    P.op('act', (lambda d=d: lambda e: e.activation(out=egc[d][:], in_=psh[0:C, d * 32:(d + 1) * 32], func=AF.Exp))(), reads=[('ps', 5)], writes=[('egc', d)])
                    P.op('dve', (lambda d=d: lambda e: e.tensor_scalar(out=negc[d][:], in0=egc[d][:], scalar1=-1.0, scalar2=None, op0=ALU.mult))(), reads=[('egc', d)], writes=[('negc', d)])
                    P.op('act', (lambda d=d: lambda e: e.activation(out=egl[d][:], in_=psh[:, 64 + d * 32:64 + (d + 1) * 32], func=AF.Exp))(), reads=[('ps', 5)], writes=[('egl', d)])
                    P.op('dve', (lambda d=d: lambda e: e.tensor_tensor(out=kdf[d][:], in0=psh[0:C, 64 + d * 32:64 + (d + 1) * 32], in1=gc[d][:], op=ALU.subtract))(), reads=[('ps', 5), ('gc', d)], writes=[('kdf', d)])
                    P.op('act', (lambda d=d: lambda e: e.activation(out=kdf[d][:], in_=kdf[d][:], func=AF.Exp))(), reads=[('kdf', d)], writes=[('kdf', d)])
                units = []
                for kh in range(16):
                    for d in range(2):
                        for hv in (2 * kh, 2 * kh + 1):
                            units.append((d, kh, hv))
                kkdone = set()
                for b0 in range(0, len(units), NSL):
                    batch = units[b0:b0 + NSL]
                    slots = []
                    for (d, kh, hv) in batch:
                        sl_ = slot_ctr[0] % NSL
                        slot_ctr[0] += 1
                        slots.append(sl_)
                    U = list(zip(batch, slots))
                    pu = lambda sl_: self.ps[sl_]
                    for (d, kh, hv), sl_ in U:
                        P.op('pool', (lambda d=d, hv=hv, sl_=sl_, lb=lb: lambda e: e.tensor_scalar(out=gTri[sl_][:], in0=tri[d], scalar1=gg[d][lb][:, hv:hv + 1], scalar2=None, op0=ALU.mult))(),
                             reads=['cst', ('gg', d, lb)], writes=[('gTri', sl_)])
                        P.op('pe', (lambda sl_=sl_: lambda e: e.matmul(pu(sl_)[0:C, 0:32], self.cst[0:C, 1, 0:C], gTri[sl_][:], start=True, stop=True))(),
                             reads=['cst', ('gTri', sl_)], writes=[('ps', sl_)])
                    for (d, kh, hv), sl_ in U:
                        P.op('dve', (lambda d=d, hv=hv, sl_=sl_: lambda e: e.scalar_tensor_tensor(out=Dm[sl_][:], in0=pu(sl_)[0:C, 0:32], scalar=gc[d][:, hv:hv + 1], in1=negm[d], op0=ALU.subtract, op1=ALU.add))(),
                             reads=[('ps', sl_), ('gc', d), 'cst'], writes=[('Dm', sl_)])
                        P.op('act', (lambda sl_=sl_: lambda e: e.activation(out=Em[sl_][:], in_=Dm[sl_][:], func=AF.Exp))(), reads=[('Dm', sl_)], writes=[('Em', sl_)])
                    for (d, kh, hv), sl_ in U:
                        P.op('pe', (lambda d=d, kh=kh, sl_=sl_, lb=lb: lambda e: e.matmul(pu(sl_)[0:C, 64:128], kq[d][lb][:, kh, 0, :], kq[d][lb][:, kh, :, :].rearrange("p a c -> p (a c)"), start=True, stop=True))(),
                             reads=[('kq', d, lb, 0), ('kq', d, lb, 1)], writes=[('ps', sl_)])
                    for (d, kh, hv), sl_ in U:
                        co = 64
                        kk = ('ps', sl_)
                        P.op('pool', (lambda sl_=sl_: lambda e: e.tensor_tensor(out=EmS[sl_][:], in0=Em[sl_][:], in1=offd, op=ALU.mult))(), reads=[('Em', sl_), 'cst'], writes=[('EmS', sl_)])
                        P.op('dve', (lambda sl_=sl_, co=co: lambda e: e.tensor_tensor(out=AT[sl_][:], in0=pu(sl_)[0:C, co + 32:co + 64], in1=Em[sl_][:], op=ALU.mult))(), reads=[kk, ('Em', sl_)], writes=[('AT', sl_)])
                        P.op('dve', (lambda d=d, hv=hv, sl_=sl_, co=co, lb=lb: lambda e: e.scalar_tensor_tensor(out=Mp[sl_][:], in0=pu(sl_)[0:C, co:co + 32], scalar=bb[d][lb][:, hv:hv + 1], in1=EmS[sl_][:], op0=ALU.mult, op1=ALU.mult))(),
                             reads=[kk, ('bb', d, lb), ('EmS', sl_)], writes=[('Mp', sl_)])
                    for (d, kh, hv), sl_ in U:
                        P.op('pool', (lambda sl_=sl_: lambda e: e.tensor_tensor(out=Rb[sl_][0][:], in0=ident, in1=Mp[sl_][:], op=ALU.subtract))(), reads=['cst', ('Mp', sl_)], writes=[('Rb', sl_, 0)])
                        P.op('pe', (lambda sl_=sl_: lambda e: e.transpose(pu(sl_)[0:C, 32:64], Mp[sl_][:], ident))(), reads=[('Mp', sl_), 'cst'], writes=[('ps', sl_)])
                        P.op('act', (lambda sl_=sl_: lambda e: e.activation(out=Nn[sl_][:], in_=pu(sl_)[0:C, 32:64], func=AF.Copy))(), reads=[('ps', sl_)], writes=[('Nn', sl_)])
                    for k in range(1, 5):
                        for (d, kh, hv), sl_ in U:
                            if k == 1:
                                Pp, Qp, kP, kQ = Mp[sl_][:], Nn[sl_][:], ('Mp', sl_), ('Nn', sl_)
                            else:
                                Pp, Qp = PQ[sl_][(k - 1) % 2][:, 0:32], PQ[sl_][(k - 1) % 2][:, 32:64]
                                kP = kQ = ('PQ', sl_, (k - 1) % 2)
                            if k < 4:
                                P.op('pe', (lambda sl_=sl_, Pp=Pp, Qp=Qp: lambda e: e.matmul(pu(sl_)[0:C, 128:160], Qp, Pp, start=True, stop=True))(), reads=[kP, kQ], writes=[('ps', sl_)])
                            P.op('pe', (lambda sl_=sl_, Pp=Pp, Qp=Qp: lambda e: e.matmul(pu(sl_)[0:C, 160:192], Pp, Qp, start=True, stop=True))(), reads=[kP, kQ], writes=[('ps', sl_)])
                            lo = 128 if k < 4 else 160
                            P.op('act', (lambda sl_=sl_, k=k, lo=lo: lambda e: e.activation(out=PQ[sl_][k % 2][:, lo - 128:64], in_=pu(sl_)[0:C, lo:192], func=AF.Copy))(),
                                 reads=[('ps', sl_)], writes=[('PQ', sl_, k % 2)])
                        for (d, kh, hv), sl_ in U:
                            P.op('pe', (lambda sl_=sl_, k=k: lambda e: e.matmul(pu(sl_)[0:C, 192:224], PQ[sl_][k % 2][:, 32:64], Rb[sl_][(k - 1) % 2][:], start=True, stop=True))(),
                                 reads=[('PQ', sl_, k % 2), ('Rb', sl_, (k - 1) % 2)], writes=[('ps', sl_)])
                            P.op('dve', (lambda sl_=sl_, k=k: lambda e: e.tensor_tensor(out=Rb[sl_][k % 2][:], in0=pu(sl_)[0:C, 192:224], in1=Rb[sl_][(k - 1) % 2][:], op=ALU.add))(),
                                 reads=[('ps', sl_), ('Rb', sl_, (k - 1) % 2)], writes=[('Rb', sl_, k % 2)])
                    for (d, kh, hv), sl_ in U:
                        u = d * 32 + hv
                        P.op('pe', (lambda d=d, kh=kh, u=u, sl_=sl_, lb=lb: lambda e: e.matmul(pu(sl_)[0:C, 256:384], kq[d][lb][:, kh, 0, :], S[:, u, :], start=True, stop=True))(),
                             reads=[('kq', d, lb, 0), ('S', u)], writes=[('ps', sl_)])
                        P.op('dve', (lambda d=d, hv=hv, sl_=sl_, lb=lb: lambda e: e.scalar_tensor_tensor(out=rhs0[sl_][:], in0=pu(sl_)[0:C, 256:384], scalar=negc[d][:, hv:hv + 1], in1=vtm[d][lb][:, hv * 128:(hv + 1) * 128], op0=ALU.mult, op1=ALU.add))(),
                             reads=[('ps', sl_), ('negc', d), ('vtm', d, lb)], writes=[('rhs0', sl_)])
                    for (d, kh, hv), sl_ in U:
                        P.op('pe', (lambda sl_=sl_: lambda e: e.matmul(pu(sl_)[0:C, 384:512], Rb[sl_][0][:], rhs0[sl_][:], start=True, stop=True))(),
                             reads=[('Rb', sl_, 0), ('rhs0', sl_)], writes=[('ps', sl_)])
                        P.op('dve', (lambda d=d, hv=hv, sl_=sl_, lb=lb: lambda e: e.tensor_scalar(out=vn[sl_][:], in0=pu(sl_)[0:C, 384:512], scalar1=bb[d][lb][:, hv:hv + 1], scalar2=None, op0=ALU.mult))(),
                             reads=[('ps', sl_), ('bb', d, lb)], writes=[('vn', sl_)])
                    for (d, kh, hv), sl_ in U:
                        u = d * 32 + hv
                        P.op('pe', (lambda d=d, kh=kh, u=u, sl_=sl_, lb=lb: lambda e: e.matmul(pu(sl_)[0:C, 0:128], kq[d][lb][:, kh, 1, :], S[:, u, :], start=True, stop=True))(),
                             reads=[('kq', d, lb, 1), ('S', u)], writes=[('ps', sl_), ('ps', sl_), ('ps', sl_)])
                        P.op('pe', (lambda sl_=sl_: lambda e: e.matmul(pu(sl_)[0:C, 128:256], AT[sl_][:], vn[sl_][:], start=True, stop=True))(),
                             reads=[('AT', sl_), ('vn', sl_)], writes=[('ps', sl_), ('ps', sl_)])
                        P.op('act', (lambda sl_=sl_: lambda e: e.activation(out=t2[sl_][:], in_=pu(sl_)[0:C, 128:256], func=AF.Copy))(), reads=[('ps', sl_)], writes=[('t2', sl_)])
                        P.op('dve', (lambda d=d, hv=hv, sl_=sl_: lambda e: e.scalar_tensor_tensor(out=ost[d][:, hv * 128:(hv + 1) * 128], in0=pu(sl_)[0:C, 0:128], scalar=egc[d][:, hv:hv + 1], in1=t2[sl_][:], op0=ALU.mult, op1=ALU.add))(),
                             reads=[('ps', sl_), ('egc', d), ('t2', sl_)], writes=[('ost', d, hv)])
                        P.op('pool', (lambda d=d, kh=kh, hv=hv, sl_=sl_, lb=lb: lambda e: e.tensor_scalar(out=kdec[sl_][:], in0=ktm[d][lb][:, kh * 128:(kh + 1) * 128], scalar1=kdf[d][:, hv:hv + 1], scalar2=None, op0=ALU.mult))(),
                             reads=[('ktm', d, lb), ('kdf', d)], writes=[('kdec', sl_)])
                        P.op('pe', (lambda sl_=sl_: lambda e: e.matmul(pu(sl_)[:, 256:384], kdec[sl_][:], vn[sl_][:], start=True, stop=True))(),
                             reads=[('kdec', sl_), ('vn', sl_)], writes=[('ps', sl_)])
                        P.op('dve', (lambda d=d, hv=hv, u=u, sl_=sl_: lambda e: e.scalar_tensor_tensor(out=S[:, u, :], in0=S[:, u, :], scalar=egl[d][:, hv:hv + 1], in1=pu(sl_)[:, 256:384], op0=ALU.mult, op1=ALU.add))(),
                             reads=[('S', u), ('egl', d), ('ps', sl_)], writes=[('S', u)])
                for d in range(2):
                    ci = order[d][s_]
                    dst = self.OF if d == 0 else self.OB
                    self.dma(dst[ci * C:(ci + 1) * C, :], ost[d][:], reads=[('ost', d, hv) for hv in range(32)], writes=[('O', d, ci)])


def _consts():
    c = np.zeros((128, 8, 128), np.float32)
    idx = np.arange(128)
    c[:, 0, :] = np.eye(128)
    c[:, 1, :] = 1.0
    c[:, 2, :] = (idx[:, None] <= idx[None, :])
    c[:, 3, :] = (idx[:, None] >= idx[None, :])
    c[:, 4, :] = np.where(idx[:, None] <= idx[None, :], 0.0, NEG)
    c[:, 5, :] = np.where(idx[:, None] >= idx[None, :], 0.0, NEG)
    c[:, 6, :] = 1.0 - np.eye(128)
    return c


def _na_bias_tables(rpb):
    L = rpb.shape[0]
    out = np.full((L, 16, 21, 128, 128), NEG, np.float32)
    cols = np.arange(64)
    cs = np.clip(cols - 8, 0, 48)
    inwin = (cols[None, :] >= cs[:, None]) & (cols[None, :] < cs[:, None] + 16)
    dc = np.clip(cols[None, :] - cols[:, None], -15, 15) + 15

    def fill(blk, rp, ktile):
        for qrl in range(2):
            r = 2 * rp + qrl
            rs = min(max(r - 4, 0), 56)
            for krl in range(2):
                jr = 2 * ktile + krl
                if jr < rs or jr >= rs + 8:
                    continue
                dr = jr - r + 7
                vals = rpb[:, :, dr, :][:, :, dc]
                vals = np.where(inwin[None, None], vals, NEG)
                out[:, :, blk, krl * 64:(krl + 1) * 64, qrl * 64:(qrl + 1) * 64] = vals.transpose(0, 1, 3, 2)

    for kk in range(5):
        fill(kk, 10, 8 + kk)
    for kk in range(4):
        fill(5 + kk, 0, kk)
        fill(9 + kk, 1, kk)
        fill(13 + kk, 30, 28 + kk)
        fill(17 + kk, 31, 28 + kk)
    return np.ascontiguousarray(out.transpose(0, 1, 3, 2, 4))


def kernel(**inputs):
    nl = int(os.environ.get("MK_NL", "4"))
    f = lambda k: np.ascontiguousarray(np.asarray(inputs[k], dtype=np.float32))
    x, c, ctx, c_ctx = f("x"), f("c"), f("ctx"), f("c_ctx")
    nna = (nl + 1) // 2
    ngd = nl // 2
    shared = {
        "b_ada": f("b_ada"), "norm1_w": f("norm1_w"), "norm2_w": f("norm2_w"),
        "na_qg": f("na_q_gain").reshape(2, 128, 1), "na_kg": f("na_k_gain").reshape(2, 128, 1),
        "gdn_conv": np.ascontiguousarray(f("gdn_conv_w").reshape(2, 4, 64, 128).transpose(0, 3, 2, 1)),
        "gdn_wba": np.ascontiguousarray(f("gdn_w_ba").transpose(0, 2, 1, 3).reshape(2, D, 128)),
        "gdn_a_log": f("gdn_a_log").reshape(2, 64), "gdn_dt_bias": f("gdn_dt_bias").reshape(2, 64),
        "gdn_norm_w": f("gdn_norm_w"),
        "consts": _consts(),
    }
    nab = _na_bias_tables(f("na_rpb"))
    for i in range(nl):
        shared["w_ada%d" % i] = f("w_ada")[i]
        shared["ffn_w1%d" % i] = f("ffn_w1")[i]
        shared["ffn_w3%d" % i] = f("ffn_w3")[i]
        shared["ffn_w2%d" % i] = f("ffn_w2")[i]
    for jj in range(nna):
        shared["na_w_qkv%d" % jj] = f("na_w_qkv")[jj]
        shared["na_w_o%d" % jj] = f("na_w_o")[jj]
        shared["na_bias%d" % jj] = nab[jj]
    for jj in range(ngd):
        shared["gdn_w_in%d" % jj] = f("gdn_w_in")[jj]
        shared["gdn_w_o%d" % jj] = f("gdn_w_o")[jj]
    bld = Builder(nl)
    in_maps = []
    for b in range(4):
        m = {}
        xin = np.ascontiguousarray(np.concatenate([x[b], ctx[b]], axis=0))
        cv = np.ascontiguousarray(np.stack([c[b], c_ctx], axis=-1).reshape(16, 128, 2).transpose(1, 0, 2))
        for n in bld.in_names:
            shp = bld.in_shapes[n]
            if shp == [1, 1]:
                m[n] = np.zeros((1, 1), np.float32)
            elif n == "x_in":
                m[n] = xin
            elif n == "cvec":
                m[n] = cv
            else:
                m[n] = shared[n]
                assert list(m[n].shape) == shp, (n, m[n].shape, shp)
        in_maps.append(m)
    import time as _t
    _t0 = _t.time()
    nc = bld.build()
    _t1 = _t.time()
    res = run_bass_kernel_spmd(nc, in_maps, core_ids=list(range(4)))
    print("[kernel] build %.1fs run %.1fs in_bytes/core %.0fMB" % (_t1 - _t0, _t.time() - _t1, sum(v.nbytes for v in in_maps[0].values()) / 1e6), flush=True)
    return np.stack([np.asarray(r["out"], dtype=np.float32) for r in res.results], axis=0)
```
